# Optimizing a Trainium2 kernel written in Bass

```python
import jax, jax.numpy as jnp
from jax import lax
import numpy as np

D_MODEL = 4096
BATCH = 4
SEQ = 2048
DEPTH = 1

MEM_LEN = 256
EPS = 1e-6
ROPE_THETA = 500000.0
CHUNK = 128
A_GROUPS = 16
A_WIDTH = D_MODEL // 2
A_GROUP_DIM = A_WIDTH // A_GROUPS
B_HEADS = 16
B_HEAD_DIM = 128
B_KV_HEADS = 4
B_WIDTH = B_HEADS * B_HEAD_DIM
B_ROT = B_HEAD_DIM // 4
IDX_HEADS = 16
IDX_DIM = 64
IDX_ROT = IDX_DIM // 4
TOPK_MAX = 256
Q_BLOCK = 128
X_HEADS = 4
X_HEAD_DIM = 256
X_WIDTH = X_HEADS * X_HEAD_DIM
FFN_HIDDEN = -(-8 * D_MODEL // (3 * 256)) * 256
IN_SIZES = (2 * A_WIDTH, B_WIDTH, B_KV_HEADS * B_HEAD_DIM, B_KV_HEADS * B_HEAD_DIM,
            IDX_HEADS * IDX_DIM, IDX_DIM, IDX_HEADS, 2 * D_MODEL)
IN_WIDTH = sum(IN_SIZES)

kernel_name = 'hybrid_gmlp_dsa_gated_block'


def rmsnorm(x, g):
    xf = x.astype(jnp.float32)
    y = xf * lax.rsqrt(jnp.mean(xf * xf, axis=-1, keepdims=True) + EPS)
    return (y * g.astype(jnp.float32)).astype(x.dtype)


def rope_angles(positions, rot_dim):
    inv_freq = ROPE_THETA ** (-jnp.arange(0, rot_dim, 2, dtype=jnp.float32) / rot_dim)
    ang = positions.astype(jnp.float32)[..., None] * inv_freq
    return jnp.cos(ang), jnp.sin(ang)


def apply_partial_rope(x, cos, sin):
    r = 2 * cos.shape[-1]
    xr, xp = x[..., :r], x[..., r:]
    x1, x2 = xr[..., : r // 2], xr[..., r // 2:]
    c = cos[:, :, None, :].astype(x.dtype)
    s = sin[:, :, None, :].astype(x.dtype)
    return jnp.concatenate([x1 * c - x2 * s, x2 * c + x1 * s, xp], axis=-1)


def chunked_spatial_gating(z, norm_g, w_s, b_s):
    bsz, seq, _ = z.shape
    u, v = jnp.split(z, 2, axis=-1)
    v = rmsnorm(v, norm_g)
    v = v.reshape(bsz, seq // CHUNK, CHUNK, A_GROUPS, A_GROUP_DIM)
    causal = jnp.tril(jnp.ones((CHUNK, CHUNK), dtype=bool))
    w = jnp.where(causal[None], w_s, 0).astype(v.dtype)
    s = jnp.einsum('gts,bcsgd->bctgd', w, v) + b_s.T.astype(v.dtype)[None, None, :, :, None]
    return u * s.reshape(bsz, seq, A_WIDTH)


def dsa_attention(q, k, v, qi, ki, wi):
    bsz, seq = q.shape[0], q.shape[1]
    n_sel = min(TOPK_MAX, seq // 4)
    n_blocks = seq // Q_BLOCK
    grp = B_HEADS // B_KV_HEADS
    scale = B_HEAD_DIM ** -0.5
    idx_scale = (IDX_DIM ** -0.5) * (IDX_HEADS ** -0.5)
    key_pos = jnp.arange(seq)
    gather = jax.vmap(lambda table, ids: table[ids])

    def block(bi):
        start = bi * Q_BLOCK
        qb = lax.dynamic_slice_in_dim(q, start, Q_BLOCK, axis=1)
        qib = lax.dynamic_slice_in_dim(qi, start, Q_BLOCK, axis=1)
        wib = lax.dynamic_slice_in_dim(wi, start, Q_BLOCK, axis=1)
        qpos = start + jnp.arange(Q_BLOCK)
        causal = key_pos[None, :] <= qpos[:, None]
        dots = jnp.einsum('bthd,bsd->bths', qib, ki).astype(jnp.float32)
        iscore = jnp.einsum('bth,bths->bts', wib.astype(jnp.float32), jax.nn.relu(dots)) * idx_scale
        iscore = jnp.where(causal[None], iscore, -jnp.inf)
        _, sel = lax.top_k(iscore, n_sel)
        valid = sel <= qpos[None, :, None]
        ks = gather(k, sel)
        vs = gather(v, sel)
        qg = qb.reshape(bsz, Q_BLOCK, B_KV_HEADS, grp, B_HEAD_DIM)
        logits = jnp.einsum('btkgd,btskd->btkgs', qg, ks).astype(jnp.float32) * scale
        logits = jnp.where(valid[:, :, None, None, :], logits, -jnp.inf)
        p = jax.nn.softmax(logits, axis=-1).astype(v.dtype)
        o = jnp.einsum('btkgs,btskd->btkgd', p, vs)
        return o.reshape(bsz, Q_BLOCK, B_WIDTH)

    out = lax.map(block, jnp.arange(n_blocks))
    return out.transpose(1, 0, 2, 3).reshape(bsz, seq, B_WIDTH)


def memory_cross_attention(h, mem_n, wq, wk, wv, wo):
    bsz, seq, _ = h.shape
    m = mem_n.shape[1]
    q = (h @ wq).reshape(bsz, seq, X_HEADS, X_HEAD_DIM)
    k = (mem_n @ wk).reshape(bsz, m, X_HEADS, X_HEAD_DIM)
    v = (mem_n @ wv).reshape(bsz, m, X_HEADS, X_HEAD_DIM)
    logits = jnp.einsum('bshd,bmhd->bhsm', q, k).astype(jnp.float32) * (X_HEAD_DIM ** -0.5)
    p = jax.nn.softmax(logits, axis=-1).astype(v.dtype)
    o = jnp.einsum('bhsm,bmhd->bshd', p, v).reshape(bsz, seq, X_WIDTH)
    return o @ wo


def setup_inputs(seed: int = 0) -> dict:
    key = jax.random.key(seed)
    ks = jax.random.split(key, 24)

    def nrm(k, shape, scale):
        return jax.random.normal(k, shape, jnp.float32) * scale

    def gain(k, shape):
        return 1.0 + 0.02 * jax.random.normal(k, shape, jnp.float32)

    L = DEPTH
    x = jax.random.normal(ks[0], (BATCH, SEQ, D_MODEL), jnp.float32)
    mem = jax.random.normal(ks[1], (BATCH, MEM_LEN, D_MODEL), jnp.float32)
    offsets = jax.random.randint(ks[2], (BATCH, 1), 0, 4096, dtype=jnp.int32)
    positions = (offsets + jnp.arange(SEQ, dtype=jnp.int32)[None, :]).astype(jnp.int32)
    return {
        'x': x,
        'mem': mem,
        'positions': positions,
        'norm_mix_g': gain(ks[3], (L, D_MODEL)),
        'w_in': nrm(ks[4], (L, D_MODEL, IN_WIDTH), D_MODEL ** -0.5),
        'a_norm_g': gain(ks[5], (L, A_WIDTH)),
        'a_spatial_w': nrm(ks[6], (L, A_GROUPS, CHUNK, CHUNK), CHUNK ** -0.5),
        'a_spatial_b': gain(ks[7], (L, A_GROUPS, CHUNK)),
        'p_a': nrm(ks[8], (L, A_WIDTH, D_MODEL), A_WIDTH ** -0.5),
        'p_b': nrm(ks[9], (L, B_WIDTH, D_MODEL), B_WIDTH ** -0.5),
        'w_out': nrm(ks[10], (L, D_MODEL, D_MODEL), D_MODEL ** -0.5),
        'norm_x_g': gain(ks[11], (L, D_MODEL)),
        'norm_mem_g': gain(ks[12], (L, D_MODEL)),
        'xq_w': nrm(ks[13], (L, D_MODEL, X_WIDTH), D_MODEL ** -0.5),
        'xk_w': nrm(ks[14], (L, D_MODEL, X_WIDTH), D_MODEL ** -0.5),
        'xv_w': nrm(ks[15], (L, D_MODEL, X_WIDTH), D_MODEL ** -0.5),
        'xo_w': nrm(ks[16], (L, X_WIDTH, D_MODEL), X_WIDTH ** -0.5),
        'norm_ffn_g': gain(ks[17], (L, D_MODEL)),
        'ffn_w1': nrm(ks[18], (L, D_MODEL, FFN_HIDDEN), D_MODEL ** -0.5),
        'ffn_w3': nrm(ks[19], (L, D_MODEL, FFN_HIDDEN), D_MODEL ** -0.5),
        'ffn_w2': nrm(ks[20], (L, FFN_HIDDEN, D_MODEL), FFN_HIDDEN ** -0.5),
        'final_norm_g': gain(ks[21], (D_MODEL,)),
    }


def reference(x, mem, positions, norm_mix_g, w_in, a_norm_g, a_spatial_w, a_spatial_b,
              p_a, p_b, w_out, norm_x_g, norm_mem_g, xq_w, xk_w, xv_w, xo_w,
              norm_ffn_g, ffn_w1, ffn_w3, ffn_w2, final_norm_g):
    bsz, seq, _ = x.shape
    split_points = []
    acc = 0
    for sz in IN_SIZES[:-1]:
        acc += sz
        split_points.append(acc)
    cos_b, sin_b = rope_angles(positions, B_ROT)
    cos_i, sin_i = rope_angles(positions, IDX_ROT)

    for l in range(DEPTH):
        h = rmsnorm(x, norm_mix_g[l])
        proj = h @ w_in[l]
        za, q, k, v, qi, ki, wi, gates = jnp.split(proj, split_points, axis=-1)
        y_a = chunked_spatial_gating(jax.nn.gelu(za), a_norm_g[l], a_spatial_w[l], a_spatial_b[l])
        q = apply_partial_rope(q.reshape(bsz, seq, B_HEADS, B_HEAD_DIM), cos_b, sin_b)
        k = apply_partial_rope(k.reshape(bsz, seq, B_KV_HEADS, B_HEAD_DIM), cos_b, sin_b)
        v = v.reshape(bsz, seq, B_KV_HEADS, B_HEAD_DIM)
        qi = apply_partial_rope(qi.reshape(bsz, seq, IDX_HEADS, IDX_DIM), cos_i, sin_i)
        ki = apply_partial_rope(ki[:, :, None, :], cos_i, sin_i)[:, :, 0, :]
        y_b = dsa_attention(q, k, v, qi, ki, wi)
        g_a, g_b = jnp.split(jax.nn.sigmoid(gates), 2, axis=-1)
        merged = g_a * (y_a @ p_a[l]) + g_b * (y_b @ p_b[l])
        x = x + merged @ w_out[l]
        x = x + memory_cross_attention(rmsnorm(x, norm_x_g[l]), rmsnorm(mem, norm_mem_g[l]),
                                       xq_w[l], xk_w[l], xv_w[l], xo_w[l])
        h = rmsnorm(x, norm_ffn_g[l])
        x = x + (jax.nn.silu(h @ ffn_w1[l]) * (h @ ffn_w3[l])) @ ffn_w2[l]

    return rmsnorm(x, final_norm_g)
```

```python
import numpy as np
import concourse.bass as bass
import concourse.mybir as mybir
from concourse.bass_utils import run_bass_kernel_spmd

F32 = mybir.dt.float32
BF16 = mybir.dt.bfloat16
I32 = mybir.dt.int32
AF = mybir.ActivationFunctionType
ALU = mybir.AluOpType
AX = mybir.AxisListType


class Sched:
    ENGS = ("pe", "act", "dve", "pool", "sp")
    NDMA = 8
    SEM_MAX = 30000

    def __init__(self, nc):
        self.nc = nc
        self.ops = []
        self.last_w = {}
        self.readers = {}
        self.out_dmas = []
        self.bar_deps = set()
        self.lastx = {}

    def op(self, eng, fn, r=(), w=(), dma=False, deps=()):
        i = len(self.ops)
        d = set(deps) | self.bar_deps
        isps = lambda t: isinstance(t, tuple) and t[0] == "ps"
        xs = [t for t in list(r) + list(w) if isps(t)]
        r = [t for t in r if not isps(t)]
        w = [t for t in w if not isps(t)]
        for t in xs:
            lx = self.lastx.setdefault(t, {})
            for e2, j in lx.items():
                if e2 != eng:
                    d.add(j)
            lx[eng] = i
        for t in r:
            if t in self.last_w:
                d.add(self.last_w[t])
        for t in w:
            if t in self.last_w:
                d.add(self.last_w[t])
            for x in self.readers.get(t, ()):
                d.add(x)
        d.discard(i)
        self.ops.append([eng, fn, d, dma])
        for t in r:
            self.readers.setdefault(t, []).append(i)
        for t in w:
            self.last_w[t] = i
            self.readers[t] = []
        return i

    def dma(self, q, out, in_, r=(), w=(), is_out=False):
        i = self.op(q, lambda e: e.dma_start(out=out, in_=in_), r=r, w=w, dma=True)
        if is_out:
            self.out_dmas.append(i)
        return i

    def emit(self, stack):
        nc = self.nc
        ops = self.ops
        n = len(ops)
        need = [False] * n
        for i, (eng, fn, deps, dma) in enumerate(ops):
            for d in deps:
                if not ops[d][3]:
                    if ops[d][0] == eng and eng in ("pe", "sp"):
                        continue
                    need[d] = True
        sems = {e: [] for e in self.ENGS}
        cnt = {e: 0 for e in self.ENGS}
        sig = [None] * n
        dsems = {e: [stack.enter_context(nc.semaphore(f"dq_{e}_{k}")) for k in range(self.NDMA)]
                 for e in ("sp", "pool", "act")}
        dcount = {e: 0 for e in dsems}
        dlist = {e: [] for e in dsems}
        dprev = {}
        for i, (eng, fn, deps, dma) in enumerate(ops):
            if dma:
                k = dcount[eng]
                dcount[eng] += 1
                sig[i] = (dsems[eng][k % self.NDMA], 16 * (k // self.NDMA + 1))
                if k >= self.NDMA:
                    dprev[i] = dlist[eng][k - self.NDMA]
                dlist[eng].append(i)
            elif need[i]:
                if not sems[eng] or cnt[eng] >= self.SEM_MAX:
                    sems[eng].append(stack.enter_context(nc.semaphore(f"pg_{eng}_{len(sems[eng])}")))
                    cnt[eng] = 0
                cnt[eng] += 1
                sig[i] = (sems[eng][-1], cnt[eng])
        progs = {e: [] for e in self.ENGS}
        for i, (eng, fn, deps, dma) in enumerate(ops):
            progs[eng].append(i)
        last_all = [i for i in range(n) if ops[i][3]]

        def run(eng_name):
            def body(e):
                waited = {}
                for i in progs[eng_name]:
                    _, fn, deps, dma = ops[i]
                    wl = {}
                    dl = set(deps)
                    if dma and i in dprev:
                        dl.add(dprev[i])
                    for d in dl:
                        if sig[d] is None:
                            continue
                        s, v = sig[d]
                        key = id(s)
                        if waited.get(key, (None, 0))[1] >= v:
                            continue
                        if key not in wl or wl[key][1] < v:
                            wl[key] = (s, v)
                    for key, (s, v) in wl.items():
                        e.wait_ge(s, v)
                        waited[key] = (s, v)
                    ins = fn(e)
                    if sig[i] is not None:
                        s, v = sig[i]
                        ins.then_inc(s, 16 if dma else 1)
                if eng_name == "sp":
                    fin = {}
                    for i in last_all:
                        s, v = sig[i]
                        if fin.get(id(s), (None, 0))[1] < v:
                            fin[id(s)] = (s, v)
                    for key, (s, v) in fin.items():
                        if waited.get(key, (None, 0))[1] < v:
                            e.wait_ge(s, v)
            return body

        block = stack.enter_context(nc.Block())
        block.tensor(run("pe"))
        block.scalar(run("act"))
        block.vector(run("dve"))
        block.gpsimd(run("pool"))
        block.sync(run("sp"))

    def barrier(self):
        deps = set()
        lastc = {}
        dq = {}
        for i, (eng, fn, d, dma) in enumerate(self.ops):
            if dma:
                dq.setdefault(eng, []).append(i)
            else:
                lastc[eng] = i
        deps |= set(lastc.values())
        for q, l in dq.items():
            deps |= set(l[-self.NDMA:])
        self.bar_deps = deps


SZ = {F32: 4, BF16: 2, I32: 4}
U8 = mybir.dt.uint8

D = 4096
NTOK = 1024
FFN = 11008
C_ZA, C_Q, C_K, C_V, C_QI, C_KI, C_WI, C_G = 0, 4096, 6144, 6656, 7168, 8192, 8256, 8272
NEG = -1.0e30
TWO_PI = 6.283185307179586
PI = 3.141592653589793


def carve(region, off, shape, dt):
    nb = int(np.prod(shape[1:])) * SZ[dt]
    v = region[:, off:off + nb].bitcast(dt)
    if len(shape) == 3:
        v = v.rearrange("p (a b) -> p a b", a=shape[1])
    elif len(shape) == 4:
        v = v.rearrange("p (a b c) -> p a b c", a=shape[1], b=shape[2])
    return v


def build_program(dbg=(), stop_after=None):
    from contextlib import ExitStack
    nc = bass.Bass("TRN2", target_bir_lowering=False)

    def din(name, shape, dt=F32):
        return nc.dram_tensor(name, shape, dt, kind="ExternalInput").ap()

    def dscr(name, shape, dt):
        return nc.dram_tensor(name, shape, dt, kind=("ExternalOutput" if name in dbg else "Internal")).ap()

    x_full = din("x_full", [2048, D]); x_own = din("x_own", [NTOK, D]); mem = din("mem", [256, D])
    pos = din("pos", [128, 24], I32); maskb = din("maskb", [128, 256]); tril = din("tril", [128, 128])
    invf = din("invf", [128, 24])
    g_mix = din("norm_mix_g", [32, 128]); g_x = din("norm_x_g", [32, 128]); g_mem = din("norm_mem_g", [32, 128])
    g_ffn = din("norm_ffn_g", [32, 128]); g_fin = din("final_norm_g", [1, D])
    w_in = din("w_in", [D, 16464]); a_g = din("a_norm_g", [1, 2048])
    a_sw = din("a_spatial_w", [16, 128, 128]); a_sb = din("a_spatial_b", [16, 128])
    p_a = din("p_a", [2048, D]); p_b = din("p_b", [2048, D]); w_out = din("w_out", [D, D])
    xq_w = din("xq_w", [D, 1024]); xk_w = din("xk_w", [D, 1024]); xv_w = din("xv_w", [D, 1024]); xo_w = din("xo_w", [1024, D])
    w1 = din("ffn_w1", [D, FFN]); w3 = din("ffn_w3", [D, FFN]); w2 = din("ffn_w2", [FFN, D])
    out = nc.dram_tensor("out", [NTOK, D], F32, kind="ExternalOutput").ap()

    za_s = dscr("za_s", [NTOK, 4096], BF16); q_s = dscr("q_s", [NTOK, 2048], BF16); qi_s = dscr("qi_s", [NTOK, 1024], BF16)
    gates_s = dscr("gates_s", [NTOK, 8192], BF16); ya_s = dscr("ya_s", [NTOK, 2048], BF16); yb_s = dscr("yb_s", [NTOK, 2048], BF16)
    merged_s = dscr("merged_s", [NTOK, D], BF16); x1_s = dscr("x1_s", [NTOK, D], F32); qx_s = dscr("qx_s", [NTOK, 1024], BF16)
    x2_s = dscr("x2_s", [NTOK, D], F32); fT_s = dscr("fT_s", [FFN, NTOK], BF16); x3_s = dscr("x3_s", [NTOK, D], F32)
    dbg_wi = dscr("dbg_wi", [128, 128], F32) if "dbg_wi" in dbg else None

    st = ExitStack()
    WB = st.enter_context(nc.sbuf_tensor("WB", [128, 65536], U8))
    AT8 = st.enter_context(nc.sbuf_tensor("AT8", [128, 65536], U8))
    AUX = st.enter_context(nc.sbuf_tensor("AUX", [128, 41984], U8))
    MISC = st.enter_context(nc.sbuf_tensor("MISC", [128, 30720], U8))
    PS = st.enter_context(nc.psum_tensor("PS", [128, 4096], F32))
    s = Sched(nc)

    def bank(b, width=512):
        return PS[:, b * 512:b * 512 + width]

    ident = carve(MISC, 0, [128, 128], BF16)
    identf = carve(MISC, 256, [128, 128], F32)
    cosT = carve(MISC, 768, [128, 24, 24], F32)
    sinT = carve(MISC, 3072, [128, 24, 24], F32)
    wi_sb = carve(MISC, 5376, [128, 8, 16], F32)
    gcol = carve(MISC, 5888, [128, 32], F32)
    ss = carve(MISC, 6016, [128, 16], F32)
    rstd = carve(MISC, 6080, [128, 16], F32)
    ssf = carve(MISC, 6144, [128, 64], F32)
    ropet = [carve(MISC, 6400 + 256 * i, [128, 64], F32) for i in range(4)]
    m8 = carve(MISC, 7424, [128, 8], F32)
    rden = carve(MISC, 7456, [128, 8], F32)
    bT = carve(MISC, 7488, [128, 16], F32)
    rstd8 = carve(MISC, 7552, [128, 8], F32)
    stg = [carve(MISC, 7680 + 1024 * i, [128, 512], BF16) for i in range(4)]
    stgf = [carve(MISC, 11776 + 2048 * i, [128, 512], F32) for i in range(3)]
    ldf = [carve(MISC, 17920 + 2048 * i, [128, 512], F32) for i in range(4)]
    ldb = [carve(MISC, 26112 + 1024 * i, [128, 512], BF16) for i in range(4)]
    wb = [carve(WB, 32768 * i, [128, 32, 512], BF16) for i in range(2)]
    AT = carve(AT8, 0, [128, 32, 1024], BF16)

    rr = {}

    def rot(name, n):
        v = rr.get(name, 0)
        rr[name] = v + 1
        return v % n

    tbanks = [6, 7]

    def tp_group(srcs, dst, r=(), w=(), evac=None, scale_cols=None):
        n = len(srcs)
        if n > 4:
            for o in range(0, n, 4):
                tp_group(srcs[o:o + 4], dst[:, o:o + 4, :], r=r, w=(w[o:o + 4] if (scale_cols is not None and w) else w), evac=evac,
                         scale_cols=(scale_cols[o:o + 4] if scale_cols is not None else None))
            return
        b = tbanks[rot("tb", len(tbanks))]
        bank_bf = bank(b).bitcast(BF16)
        Pin, F = srcs[0].shape[0], srcs[0].shape[1]

        def f(e):
            ins = None
            for k, sap in enumerate(srcs):
                ins = e.transpose(out=bank_bf[0:F, k * 128:k * 128 + Pin], in_=sap, identity=ident[0:Pin, 0:Pin])
            return ins
        s.op("pe", f, r=list(r) + ["ident"], w=[("ps", b)])
        if scale_cols is None:
            src_view = bank_bf[0:F, 0:n * 128].rearrange("p (a b) -> p a b", a=n)[:, :, 0:Pin]
            eng = evac or ("act" if b == tbanks[0] else "dve")
            if eng == "act":
                s.op("act", lambda e: e.copy(out=dst, in_=src_view), r=[("ps", b)], w=w)
            else:
                s.op("dve", lambda e: e.tensor_copy(out=dst, in_=src_view), r=[("ps", b)], w=w)
        else:
            for k in range(n):
                sc = scale_cols[k]
                o_ap = dst[:, k, :]
                i_ap = bank_bf[0:F, k * 128:k * 128 + Pin]
                wk = [w[k]] if w else []
                if False:
                    s.op("act", lambda e, o_ap=o_ap, i_ap=i_ap, sc=sc: e.activation(out=o_ap, in_=i_ap, func=AF.Identity, scale=sc),
                         r=[("ps", b), "gcol"], w=wk)
                else:
                    s.op("dve", lambda e, o_ap=o_ap, i_ap=i_ap, sc=sc: e.tensor_scalar(out=o_ap, in0=i_ap, scalar1=sc, scalar2=None, op0=ALU.mult),
                         r=[("ps", b), "gcol"], w=wk)

    def load_gcol(g_ap):
        gn = stgf[0][0:32, 0:128]
        s.dma("sp", gn, g_ap, w=[("stgf", 0)])
        b = tbanks[rot("tb", 2)]
        s.op("pe", lambda e: e.transpose(out=bank(b)[:, 0:32], in_=gn, identity=identf[0:32, 0:32]),
             r=[("stgf", 0), "identf"], w=[("ps", b)])
        s.op("dve", lambda e: e.tensor_copy(out=gcol[:, :], in_=bank(b)[:, 0:32]), r=[("ps", b)], w=["gcol"])

    def rstd_ops(col, n):
        c = slice(col, col + 1)
        s.op("dve", lambda e: e.tensor_scalar(out=rstd[:, c], in0=ss[:, c], scalar1=1.0 / n, scalar2=1e-6, op0=ALU.mult, op1=ALU.add),
             r=[("ss", col)], w=[("rstd", col)])
        s.op("act", lambda e: e.activation(out=rstd[:, c], in_=rstd[:, c], func=AF.Sqrt), r=[("rstd", col)], w=[("rstd", col)])
        s.op("dve", lambda e: e.reciprocal(out=rstd[:, c], in_=rstd[:, c]), r=[("rstd", col)], w=[("rstd", col)])

    def nt_tile(srcs, i, xt, hn, dstf, use_g, nrm=True, wtok=None, part="LT"):
        if "L" not in part:
            pass
        elif nrm:
            s.dma("sp", xt[i], srcs[0], w=[("xt", i)])
            s.op("act", lambda e: e.activation(out=hn[i], in_=xt[i], func=AF.Square, accum_out=ss[:, i:i + 1]),
                 r=[("xt", i)], w=[("hn", i), ("ss", i)])
            rstd_ops(i, 4096)
            s.op("act", lambda e: e.activation(out=hn[i], in_=xt[i], func=AF.Identity, scale=rstd[:, i:i + 1]),
                 r=[("xt", i), ("rstd", i)], w=[("hn", i)])
        else:
            for (dsl, sap) in srcs:
                s.dma("sp", hn[i][:, dsl], sap, w=[("hn", i)])
        if "T" not in part:
            return
        for q in range(4):
            srcl = [hn[i][:, (8 * q + k) * 128:(8 * q + k + 1) * 128] for k in range(8)]
            wt = [wtok(8 * q + k) for k in range(8)] if wtok else ()
            if use_g:
                tp_group(srcl, dstf(8 * q, 8), r=[("hn", i)], w=wt, scale_cols=[gcol[:, 8 * q + k:8 * q + k + 1] for k in range(8)])
            else:
                tp_group(srcl, dstf(8 * q, 8), r=[("hn", i)], w=wt)

    def rope_epi(psb, b, H, Dh, half, cos_ap, sin_ap, dst, dtok):
        pv = psb.rearrange("p (h d) -> p h d", h=H)
        dv = dst.rearrange("p (h d) -> p h d", h=H)
        s.op("act", lambda e: e.copy(out=dst, in_=psb), r=[("ps", b)], w=[dtok])
        c = cos_ap.unsqueeze(1).to_broadcast([128, H, half])
        sn = sin_ap.unsqueeze(1).to_broadcast([128, H, half])
        t = [ropet[i][:, 0:H * half].rearrange("p (h d) -> p h d", h=H) for i in range(4)]
        x1 = pv[:, :, 0:half]
        x2 = pv[:, :, half:2 * half]
        tt_ = lambda o, a, bb, op: (lambda e: e.tensor_tensor(out=o, in0=a, in1=bb, op=op))
        s.op("dve", tt_(t[0], x1, c, ALU.mult), r=[("ps", b), "cs"], w=[("ropet", 0)])
        s.op("dve", tt_(t[1], x2, sn, ALU.mult), r=[("ps", b), "cs"], w=[("ropet", 1)])
        s.op("dve", tt_(t[2], x2, c, ALU.mult), r=[("ps", b), "cs"], w=[("ropet", 2)])
        s.op("dve", tt_(t[3], x1, sn, ALU.mult), r=[("ps", b), "cs"], w=[("ropet", 3)])
        s.op("dve", tt_(dv[:, :, 0:half], t[0], t[1], ALU.subtract), r=[("ropet", 0), ("ropet", 1)], w=[dtok])
        s.op("dve", tt_(dv[:, :, half:2 * half], t[2], t[3], ALU.add), r=[("ropet", 2), ("ropet", 3)], w=[dtok])

    wst_cfg = {"region": AUX, "off": 0, "extra": []}

    def wload(dst, src, KCn, width, tokf):
        per = max(1, min(KCn, 4096 // width))
        toks = []
        for ci, c0 in enumerate(range(0, KCn, per)):
            n = min(per, KCn - c0)
            nsl = 2 + len(wst_cfg["extra"])
            k = rot("wst", nsl) if nsl == 2 else rot("wst4", nsl)
            if k < 2:
                sreg, soff = wst_cfg["region"], wst_cfg["off"] + 16384 * k
            else:
                sreg, soff = wst_cfg["extra"][k - 2]
            stv = carve(sreg, soff, [128, 4096], F32)[:, 0:n * width].rearrange("p (a b) -> p a b", a=n)
            s.dma("sp", stv, src[c0 * 128:(c0 + n) * 128, :].rearrange("(kc p) n -> p kc n", p=128), w=[("wst", k)])
            d = dst[:, c0:c0 + n, :]
            tk = tokf(ci)
            toks.append(tk)
            if rot("wcast", 2) == 0:
                s.op("act", lambda e, d=d, stv=stv: e.copy(out=d, in_=stv), r=[("wst", k)], w=[tk])
            else:
                s.op("dve", lambda e, d=d, stv=stv: e.tensor_copy(out=d, in_=stv), r=[("wst", k)], w=[tk])
        return toks

    def gemm_T(actT, KC, blocks, epi, ntt=8, pre=None):
        wtoks = {}

        def load(bi):
            wap, width = blocks[bi]
            wtoks[bi] = wload(wb[bi % 2][:, 0:KC, 0:width], wap, KC, width, lambda ci, sl=bi % 2: ("w", sl, ci))
        load(0)
        if len(blocks) > 1:
            load(1)
        if ntt > 4:
            halves = [(list(range(0, 4)), 0), (list(range(4, ntt)), 4)]
        else:
            halves = None
        for bi in range(len(blocks)):
            width = blocks[bi][1]
            slot = bi % 2
            hl = halves if halves else [(list(range(ntt)), 4 * (bi % 2))]
            for tts, b0 in hl:
                bset = [b0 + i for i in range(len(tts))]
                if pre:
                    pre(bi, tts)

                per = max(1, min(KC, 4096 // width))
                for ci, c0 in enumerate(range(0, KC, per)):
                    def f(e, tts=tts, bset=bset, slot=slot, width=width, c0=c0, c1=min(KC, c0 + per)):
                        ins = None
                        for kc in range(c0, c1):
                            for i, tt in enumerate(tts):
                                ins = e.matmul(bank(bset[i], width), lhsT=actT[:, kc, tt * 128:(tt + 1) * 128],
                                               rhs=wb[slot][:, kc, 0:width], start=(kc == 0), stop=(kc == KC - 1))
                        return ins
                    s.op("pe", f, r=[wtoks[bi][ci]], w=[("ps", b) for b in bset])
                for i, tt in enumerate(tts):
                    epi(bi, tt, bset[i], width)
            if bi + 2 < len(blocks):
                load(bi + 2)

    rows = lambda tt: slice(tt * 128, (tt + 1) * 128)

    s.op("pool", lambda e: e.memset(identf[:, :], 1.0), w=["identf"])
    s.op("pool", lambda e: e.affine_select(out=identf[:, :], in_=identf[:, :], pattern=[[-1, 128]], compare_op=ALU.is_equal,
                                           fill=0.0, base=0, channel_multiplier=1), r=["identf"], w=["identf"])
    s.op("pool", lambda e: e.tensor_copy(out=ident[:, :], in_=identf[:, :]), r=["identf"], w=["ident"])
    s.barrier()
    posi = carve(AT8, 0, [128, 24], I32)
    posf = carve(AT8, 128, [128, 24], F32)
    invs = carve(AT8, 256, [128, 24], F32)
    ang = carve(AT8, 512, [128, 24, 24], F32)
    angk = carve(AT8, 512 + 2304, [128, 24, 24], I32)
    angf = carve(AT8, 512 + 2 * 2304, [128, 24, 24], F32)
    angr = carve(AT8, 512 + 3 * 2304, [128, 24, 24], F32)
    s.dma("sp", posi, pos, w=["posi"])
    s.dma("sp", invs, invf, w=["invs"])
    s.op("dve", lambda e: e.tensor_copy(out=posf, in_=posi), r=["posi"], w=["posf"])
    for t in range(24):
        s.op("dve", lambda e, t=t: e.tensor_scalar(out=ang[:, t, :], in0=invs, scalar1=posf[:, t:t + 1], scalar2=None, op0=ALU.mult),
             r=["posf", "invs"], w=[("ang", t)])
    angr_ = [("ang", t) for t in range(24)]
    for which, dstT, off in (("sin", sinT, 0.0), ("cos", cosT, 0.25)):
        s.op("dve", lambda e, off=off: e.tensor_scalar(out=angk, in0=ang, scalar1=1.0 / TWO_PI, scalar2=off, op0=ALU.mult, op1=ALU.add),
             r=angr_, w=["angk"])
        s.op("dve", lambda e: e.tensor_copy(out=angf, in_=angk), r=["angk"], w=["angf"])
        s.op("dve", lambda e: e.scalar_tensor_tensor(out=angr, in0=angf, scalar=-TWO_PI, in1=ang, op0=ALU.mult, op1=ALU.add),
             r=["angf"] + angr_, w=["angr"])
        s.op("dve", lambda e, off=off: e.tensor_scalar(out=angr, in0=angr, scalar1=off * TWO_PI, scalar2=PI, op0=ALU.add, op1=ALU.min),
             r=["angr"], w=["angr"])
        s.op("dve", lambda e: e.tensor_scalar(out=angr, in0=angr, scalar1=-PI, scalar2=None, op0=ALU.max), r=["angr"], w=["angr"])
        s.op("act", lambda e, dstT=dstT: e.activation(out=dstT, in_=angr, func=AF.Sin), r=["angr"], w=["cs"])
    s.barrier()
    if "dbg_cs" in dbg:
        dbg_cs = dscr("dbg_cs", [128, 2 * 576], F32)
        s.dma("sp", dbg_cs[:, 0:576], cosT.rearrange("p a b -> p (a b)"))
        s.dma("sp", dbg_cs[:, 576:1152], sinT.rearrange("p a b -> p (a b)"))
    if stop_after == "S":
        s.emit(st)
        st.close()
        return nc

    kt_s = dscr("kt_s", [128, 4, 2048], BF16)
    v_s = dscr("v_s", [2048, 512], BF16)
    kit_s = dscr("kit_s", [128, 2048], BF16)
    kiW = carve(AUX, 32768, [128, 32, 64], BF16)
    xtA = [carve(AT8, 16384 * i, [128, 4096], F32) for i in range(2)]
    hnA = [carve(AT8, 32768 + 8192 * i, [128, 4096], BF16) for i in range(2)]
    hTt = [carve(AT8, 49152 + 8192 * i, [128, 32, 128], BF16) for i in range(2)]
    load_gcol(g_mix)
    wkt = wload(wb[0], w_in[:, C_K:C_K + 512], 32, 512, lambda ci: ("w", 0, ci))
    wkt += wload(wb[1], w_in[:, C_V:C_V + 512], 32, 512, lambda ci: ("w", 1, ci))
    wkt += wload(kiW, w_in[:, C_KI:C_KI + 64], 32, 64, lambda ci: ("wki", ci))
    dbg_h = dscr("dbg_h", [128, 4096], BF16) if "dbg_h" in dbg else None
    def ntA(t, part):
        p = t % 2
        nt_tile([x_full[rows(t), :]], p, xtA, hnA, lambda k0, n, p=p: hTt[p][:, k0:k0 + n, :], True,
                wtok=lambda kc, p=p: ("hTt", p, kc), part=part)

    def mmA(t):
        p = t % 2
        bk, bv, bki = 2 * p, 2 * p + 1, 4 + p

        def fA(e, p=p, bk=bk, bv=bv, bki=bki):
            ins = None
            for kc in range(32):
                e.matmul(bank(bk), lhsT=hTt[p][:, kc, :], rhs=wb[0][:, kc, :], start=(kc == 0), stop=(kc == 31))
                e.matmul(bank(bv), lhsT=hTt[p][:, kc, :], rhs=wb[1][:, kc, :], start=(kc == 0), stop=(kc == 31))
                ins = e.matmul(bank(bki, 64), lhsT=hTt[p][:, kc, :], rhs=kiW[:, kc, :], start=(kc == 0), stop=(kc == 31))
            return ins
        s.op("pe", fA, r=[("hTt", p, kc) for kc in range(32)] + wkt, w=[("ps", bk), ("ps", bv), ("ps", bki)])
        k = rot("stg", 4)
        rope_epi(bank(bk), bk, 4, 128, 16, cosT[:, t, 0:16], sinT[:, t, 0:16], stg[k], ("stg", k))
        k3 = rot("stg", 4)
        tp_group([stg[k][:, g * 128:(g + 1) * 128] for g in range(4)], stg[k3].rearrange("p (a b) -> p a b", a=4), r=[("stg", k)], w=[("stg", k3)])
        s.dma("sp", kt_s[:, :, rows(t)], stg[k3].rearrange("p (a b) -> p a b", a=4), r=[("stg", k3)])
        k4 = rot("stg", 4)
        s.op("act", lambda e, k4=k4, bv=bv: e.copy(out=stg[k4], in_=bank(bv)), r=[("ps", bv)], w=[("stg", k4)])
        s.dma("sp", v_s[rows(t), :], stg[k4], r=[("stg", k4)])
        k2 = rot("stg", 4)
        rope_epi(bank(bki, 64), bki, 1, 64, 8, cosT[:, t, 16:24], sinT[:, t, 16:24], stg[k2][:, 0:64], ("stg", k2))
        s.op("act", lambda e, k2=k2: e.copy(out=stg[k2][:, 64:128], in_=stg[k2][:, 0:64]), r=[("stg", k2)], w=[("stg", k2)])
        tp_group([stg[k2][:, 0:128]], stg[k2][:, 128:256].unsqueeze(1), r=[("stg", k2)], w=[("stg", k2)])
        s.dma("sp", kit_s[:, rows(t)], stg[k2][:, 128:256], r=[("stg", k2)])

    NTA = 2 if stop_after == "A2" else 16
    ntA(0, "L"); ntA(0, "T"); ntA(1, "L"); ntA(1, "T")
    if NTA > 2:
        ntA(2, "L")
    for t in range(NTA):
        mmA(t)
        if t + 2 < NTA:
            ntA(t + 2, "T")
        if t + 3 < NTA:
            ntA(t + 3, "L")
    s.barrier()
    if stop_after in ("A", "A2"):
        s.emit(st)
        st.close()
        return nc

    xtW = [carve(WB, 32768 * i, [128, 4096], F32) for i in range(2)]
    hnW = [carve(WB, 32768 * i + 16384, [128, 4096], BF16) for i in range(2)]

    def prologue(src_fn, use_g, nrm=True, ntt=8, dstT=None):
        dT = AT if dstT is None else dstT
        for tt in range(ntt):
            nt_tile(src_fn(tt), tt % 2, xtW, hnW, lambda k0, n, tt=tt: dT[:, k0:k0 + n, rows(tt)], use_g, nrm=nrm)
        s.barrier()

    prologue(lambda tt: [x_own[rows(tt), :]], True)
    blocks = [(w_in[:, C_ZA + i * 512:C_ZA + (i + 1) * 512], 512) for i in range(8)]
    blocks += [(w_in[:, C_Q + i * 512:C_Q + (i + 1) * 512], 512) for i in range(4)]
    blocks += [(w_in[:, C_QI + i * 512:C_QI + (i + 1) * 512], 512) for i in range(2)]
    blocks += [(w_in[:, C_KI:C_KI + 80], 80)]
    blocks += [(w_in[:, C_G + i * 512:C_G + (i + 1) * 512], 512) for i in range(16)]

    def act_store(b, width, func, dst_ap, **kw):
        k = rot("stg", 4)
        s.op("act", lambda e: e.activation(out=stg[k][:, 0:width], in_=bank(b, width), func=func, **kw), r=[("ps", b)], w=[("stg", k)])
        s.dma("sp", dst_ap, stg[k][:, 0:width], r=[("stg", k)])

    def epi_g1(bi, tt, b, width):
        if bi < 8:
            act_store(b, 512, AF.Gelu_apprx_tanh, za_s[rows(tt), bi * 512:(bi + 1) * 512])
        elif bi < 12:
            k = rot("stg", 4)
            rope_epi(bank(b), b, 4, 128, 16, cosT[:, 16 + tt, 0:16], sinT[:, 16 + tt, 0:16], stg[k], ("stg", k))
            s.dma("sp", q_s[rows(tt), (bi - 8) * 512:(bi - 7) * 512], stg[k], r=[("stg", k)])
        elif bi < 14:
            k = rot("stg", 4)
            rope_epi(bank(b), b, 8, 64, 8, cosT[:, 16 + tt, 16:24], sinT[:, 16 + tt, 16:24], stg[k], ("stg", k))
            s.dma("sp", qi_s[rows(tt), (bi - 12) * 512:(bi - 11) * 512], stg[k], r=[("stg", k)])
        elif bi == 14:
            s.op("dve", lambda e: e.tensor_copy(out=wi_sb[:, tt, :], in_=bank(b)[:, 64:80]), r=[("ps", b)], w=["wi"])
        else:
            act_store(b, 512, AF.Sigmoid, gates_s[rows(tt), (bi - 15) * 512:(bi - 14) * 512])

    gemm_T(AT, 32, blocks, epi_g1)
    s.barrier()
    if dbg_wi is not None:
        s.dma("sp", dbg_wi, wi_sb.rearrange("p a b -> p (a b)"))
    if stop_after == "B":
        s.emit(st)
        st.close()
        return nc

    uA = [carve(AT8, 4096 * i, [128, 2048], BF16) for i in range(2)]
    vA = [carve(AT8, 8192 + 4096 * i, [128, 2048], BF16) for i in range(2)]
    vnA = [carve(AT8, 16384 + 4096 * i, [128, 2048], BF16) for i in range(2)]
    yaA = [carve(AT8, 24576 + 4096 * i, [128, 2048], BF16) for i in range(2)]
    wmT = carve(AT8, 32768, [128, 16, 128], BF16)
    ga_bc = carve(AT8, 36864, [128, 2048], F32)
    wnat = carve(AT8, 45056, [128, 16, 128], F32)
    wmask = carve(AT8, 53248, [128, 16, 128], BF16)
    s.dma("sp", wnat, a_sw.rearrange("g t s -> t g s"), w=["wnat"])
    s.dma("sp", stgf[1][:, 0:128], tril, w=[("stgf", 1)])
    s.dma("sp", ga_bc, a_g.broadcast_to([128, 2048]), w=["ga"])
    s.dma("sp", stgf[0][0:16, 0:128], a_sb, w=[("stgf", 0)])
    s.op("pe", lambda e: e.transpose(out=bank(6)[:, 0:16], in_=stgf[0][0:16, 0:128], identity=identf[0:16, 0:16]),
         r=[("stgf", 0), "identf"], w=[("ps", 6)])
    s.op("dve", lambda e: e.tensor_copy(out=bT[:, :], in_=bank(6)[:, 0:16]), r=[("ps", 6)], w=["bT"])
    s.op("dve", lambda e: e.tensor_tensor(out=wmask, in0=wnat, in1=stgf[1][:, 0:128].unsqueeze(1).to_broadcast([128, 16, 128]), op=ALU.mult),
         r=["wnat", ("stgf", 1)], w=["wmask"])
    for q in range(2):
        tp_group([wmask[:, 8 * q + k, :] for k in range(8)], wmT[:, 8 * q:8 * q + 8, :], r=["wmask"], w=["wmT"])
    for j in range(8):
        p = j % 2
        s.dma("sp", uA[p], za_s[rows(j), 0:2048], w=[("uA", p)])
        s.dma("sp", vA[p], za_s[rows(j), 2048:4096], w=[("vA", p)])
        s.op("act", lambda e, p=p: e.activation(out=vnA[p], in_=vA[p], func=AF.Square, accum_out=ss[:, 2 + p:3 + p]),
             r=[("vA", p)], w=[("vnA", p), ("ss", 2 + p)])
        rstd_ops(2 + p, 2048)
        s.op("dve", lambda e, p=p: e.scalar_tensor_tensor(out=vnA[p], in0=vA[p], scalar=rstd[:, 2 + p:3 + p], in1=ga_bc, op0=ALU.mult, op1=ALU.mult),
             r=[("vA", p), ("rstd", 2 + p), "ga"], w=[("vnA", p)])

        def fC(e, p=p):
            ins = None
            for g in range(16):
                ins = e.matmul(PS[:, g * 128:(g + 1) * 128], lhsT=wmT[:, g, :], rhs=vnA[p][:, g * 128:(g + 1) * 128], start=True, stop=True)
            return ins
        s.op("pe", fC, r=[("vnA", p), "wmT"], w=[("ps", b) for b in range(4)])
        for g in range(16):
            s.op("dve", lambda e, p=p, g=g: e.scalar_tensor_tensor(out=yaA[p][:, g * 128:(g + 1) * 128], in0=PS[:, g * 128:(g + 1) * 128],
                                                                  scalar=bT[:, g:g + 1], in1=uA[p][:, g * 128:(g + 1) * 128], op0=ALU.add, op1=ALU.mult),
                 r=[("ps", g // 4), "bT", ("uA", p)], w=[("yaA", p, g)])
        s.dma("sp", ya_s[rows(j), :], yaA[p], r=[("yaA", p, g) for g in range(16)])
    s.barrier()
    if stop_after == "C":
        s.emit(st)
        st.close()
        return nc

    qt = [carve(AT8, 4096 * i, [128, 2048], BF16) for i in range(2)]
    qit = [carve(AT8, 8192 + 2048 * i, [128, 1024], BF16) for i in range(2)]
    qT = [carve(AT8, 12288 + 4096 * i, [128, 16, 128], BF16) for i in range(2)]
    qiT = [carve(AT8, 20480 + 2048 * i, [128, 8, 128], BF16) for i in range(2)]
    isc = [carve(AT8, 24576 + 8192 * i, [128, 2048], F32) for i in range(2)]
    work = carve(AT8, 40960, [128, 2048], F32)
    m01 = [carve(AT8, 49152 + 4096 * i, [128, 2048], BF16) for i in range(2)]
    ybt = carve(AT8, 57344, [128, 2048], BF16)
    rsb = [carve(WB, 36864 + 2048 * i, [128, 512], F32) for i in range(4)]
    maskT4 = [carve(WB, 16384 * i, [128, 16, 512], BF16) for i in range(2)]
    pT = [carve(WB, 32768 + 1024 * i, [128, 512], BF16) for i in range(4)]
    mb = stgf[0][:, 0:256]
    s.dma("sp", mb, maskb, w=["mb"])
    KT = carve(AUX, 0, [128, 4, 2048], BF16)
    Vaug = carve(AUX, 16384, [128, 16, 4, 130], BF16)
    kiT = carve(AUX, 16384 + 16640, [128, 2048], BF16)
    s.op("dve", lambda e: e.memset(Vaug.rearrange("p a b c -> p (a b c)"), 1.0), w=["vaug"])
    s.dma("sp", KT, kt_s, w=["kt"])
    s.dma("sp", kiT, kit_s, w=["kit"])
    for t in range(16):
        s.dma("sp", Vaug[:, t, :, 0:128], v_s[rows(t), :].rearrange("p (g d) -> p g d", g=4), r=["vaug"], w=[("vaug", t)])
    SCALE_B = 128.0 ** -0.5
    TOPK_BISECT = True
    NBIS = 28
    bis = carve(MISC, 7584, [128, 8], F32)
    pow2 = carve(WB, 45056, [128, NBIS + 1], F32)
    bsteps = carve(WB, 45056 + 128, [128, NBIS + 1], F32)
    for k_ in range(NBIS + 1):
        s.op("dve", lambda e, k_=k_: e.memset(pow2[:, k_:k_ + 1], 2.0 ** -(k_ + 1)), w=["pow2"])

    def X_units(j):
        p = j % 2
        nkt = 2 * j + 2
        L = nkt * 128
        U = []

        def u_load():
            s.dma("sp", qt[p], q_s[rows(j), :], w=[("qt", p)])
            s.dma("sp", qit[p], qi_s[rows(j), :], w=[("qit", p)])
            for q in range(2):
                tp_group([qt[p][:, (8 * q + k) * 128:(8 * q + k + 1) * 128] for k in range(8)], qT[p][:, 8 * q:8 * q + 8, :],
                         r=[("qt", p)], w=[("qT", p)])
            tp_group([qit[p][:, k * 128:(k + 1) * 128] for k in range(8)], qiT[p][:, :, :], r=[("qit", p)], w=[("qiT", p)])
        U.append(u_load)
        nkb = (L + 511) // 512
        isct = [("isc", p, kb) for kb in range(nkb)]
        for kb in range(nkb):
            wd = min(512, L - kb * 512)
            cs = slice(kb * 512, kb * 512 + wd)
            for hh in range(16):
                def u_idx(kb=kb, wd=wd, cs=cs, hh=hh):
                    p0 = 64 * (hh % 2)
                    c = hh // 2
                    ib = 4 + rot("ib", 4)
                    s.op("pe", lambda e: e.matmul(bank(ib, wd), lhsT=qiT[p][p0:p0 + 64, c, :], rhs=kiT[p0:p0 + 64, cs], start=True, stop=True),
                         r=[("qiT", p), "kit"], w=[("ps", ib)])
                    rq = rot("rsb", 4)
                    s.op("act", lambda e: e.activation(out=rsb[rq][:, 0:wd], in_=bank(ib, wd), func=AF.Relu), r=[("ps", ib)], w=[("rsb", rq)])
                    if hh == 0:
                        s.op("dve", lambda e: e.tensor_scalar(out=isc[p][:, cs], in0=rsb[rq][:, 0:wd], scalar1=wi_sb[:, j, 0:1], scalar2=None, op0=ALU.mult),
                             r=[("rsb", rq), "wi"], w=[("isc", p, kb)])
                    else:
                        s.op("dve", lambda e: e.scalar_tensor_tensor(out=isc[p][:, cs], in0=rsb[rq][:, 0:wd], scalar=wi_sb[:, j, hh:hh + 1], in1=isc[p][:, cs],
                                                                    op0=ALU.mult, op1=ALU.add),
                             r=[("rsb", rq), "wi", ("isc", p, kb)], w=[("isc", p, kb)])
                U.append(u_idx)

        def u_bias():
            if TOPK_BISECT and j > 0:
                s.op("dve", lambda e: e.tensor_reduce(out=bis[:, 0:1], in_=isc[p][:, 0:L], axis=AX.X, op=ALU.min), r=isct, w=["blo"])
            s.op("dve", lambda e: e.tensor_tensor(out=isc[p][:, L - 256:L], in0=isc[p][:, L - 256:L], in1=mb, op=ALU.add), r=isct + ["mb"], w=isct)
        U.append(u_bias)
        if j == 0:
            def u_m0():
                s.op("dve", lambda e: e.tensor_scalar(out=m01[p][:, 0:L], in0=isc[p][:, 0:L], scalar1=-1.0e29, scalar2=None, op0=ALU.is_ge),
                     r=isct, w=[("m01", p)])
            U.append(u_m0)
        elif TOPK_BISECT:
            def u_init():
                s.op("dve", lambda e: e.max(out=m8[:, :], in_=isc[p][:, 0:L]), r=isct, w=["m8"])
                s.op("dve", lambda e: e.tensor_tensor(out=bis[:, 1:2], in0=m8[:, 0:1], in1=bis[:, 0:1], op=ALU.subtract), r=["m8", "blo"], w=["bd0"])
                s.op("dve", lambda e: e.tensor_scalar(out=bsteps[:, :], in0=pow2[:, :], scalar1=bis[:, 1:2], scalar2=None, op0=ALU.mult), r=["bd0", "pow2"], w=["bsteps"])
                s.op("dve", lambda e: e.tensor_tensor(out=bis[:, 2:3], in0=bis[:, 0:1], in1=bsteps[:, 0:1], op=ALU.add), r=["blo", "bsteps"], w=["bthr"])
            U.append(u_init)
            for it in range(NBIS):
                def u_bis(it=it):
                    s.op("act", lambda e: e.activation(out=work[:, 0:L], in_=isc[p][:, 0:L], func=AF.Sign, scale=-1.0, bias=bis[:, 2:3],
                                                       accum_out=bis[:, 3:4]), r=isct + ["bthr"], w=["work", "bsum"])
                    s.op("dve", lambda e: e.tensor_tensor(out=bis[:, 5:6], in0=bis[:, 2:3], in1=bsteps[:, it + 1:it + 2], op=ALU.subtract),
                         r=["bthr", "bsteps"], w=["bbase"])
                    s.op("dve", lambda e: e.tensor_scalar(out=bis[:, 4:5], in0=bis[:, 3:4], scalar1=float(L - 512), scalar2=None, op0=ALU.is_le),
                         r=["bsum"], w=["bm"])
                    s.op("dve", lambda e: e.scalar_tensor_tensor(out=bis[:, 2:3], in0=bis[:, 4:5], scalar=bsteps[:, it:it + 1], in1=bis[:, 5:6],
                                                                op0=ALU.mult, op1=ALU.add), r=["bm", "bsteps", "bbase"], w=["bthr"])
                U.append(u_bis)

            def u_lo():
                s.op("dve", lambda e: e.tensor_tensor(out=bis[:, 0:1], in0=bis[:, 2:3], in1=bsteps[:, NBIS:NBIS + 1], op=ALU.subtract),
                     r=["bthr", "bsteps"], w=["blo"])
            U.append(u_lo)

            def u_m1():
                s.op("dve", lambda e: e.tensor_scalar(out=m01[p][:, 0:L], in0=isc[p][:, 0:L], scalar1=bis[:, 0:1], scalar2=None, op0=ALU.is_ge),
                     r=isct + ["blo"], w=[("m01", p)])
            U.append(u_m1)
        else:
            for r_ in range(32):
                def u_round(r_=r_):
                    src_ = isc[p] if r_ == 0 else work
                    s.op("dve", lambda e: e.max(out=m8[:, :], in_=src_[:, 0:L]), r=isct + ["work"], w=["m8"])
                    if r_ < 31:
                        s.op("dve", lambda e: e.match_replace(out=work[:, 0:L], in_to_replace=m8[:, :], in_values=src_[:, 0:L], imm_value=NEG),
                             r=isct + ["m8", "work"], w=["work"])
                    else:
                        s.op("dve", lambda e: e.tensor_scalar(out=m01[p][:, 0:L], in0=isc[p][:, 0:L], scalar1=m8[:, 7:8], scalar2=None, op0=ALU.is_ge),
                             r=isct + ["m8"], w=[("m01", p)])
                U.append(u_round)
        for kt in range(nkt):
            def u_mt(kt=kt):
                tp_group([m01[p][:, kt * 128:(kt + 1) * 128]] * 4, maskT4[p][:, kt, :].rearrange("p (a b) -> p a b", a=4),
                         r=[("m01", p)], w=[("mT", p, kt)])
            U.append(u_mt)
        return U

    def Y_units(j):
        p = j % 2
        nkt = 2 * j + 2
        ob = 2
        O = PS[:, ob * 512:ob * 512 + 1024]
        steps_ = [(g, kt) for g in range(4) for kt in range(nkt)]
        pqs = {}

        def S1(i):
            g, kt = steps_[i]
            sbk = rot("sbk", 2)
            s.op("pe", lambda e: e.matmul(bank(sbk), lhsT=KT[:, g, kt * 128:(kt + 1) * 128],
                                          rhs=qT[p][:, 4 * g:4 * g + 4, :].rearrange("p a b -> p (a b)"), start=True, stop=True),
                 r=[("qT", p), "kt"], w=[("ps", sbk)])
            pq = rot("pT", 4)
            pqs[i] = pq
            s.op("act", lambda e: e.activation(out=pT[pq], in_=bank(sbk), func=AF.Exp, scale=SCALE_B), r=[("ps", sbk)], w=[("pT", pq)])
            s.op("dve", lambda e: e.tensor_tensor(out=pT[pq], in0=pT[pq], in1=maskT4[p][:, kt, :], op=ALU.mult),
                 r=[("pT", pq), ("mT", p, kt)], w=[("pT", pq)])

        def S2(i):
            g, kt = steps_[i]
            pq = pqs[i]

            def fO(e):
                ins = None
                for hq in range(4):
                    ins = e.matmul(O[:, hq * 256:hq * 256 + 129], lhsT=pT[pq][:, hq * 128:(hq + 1) * 128], rhs=Vaug[:, kt, g, 0:129],
                                   start=(kt == 0 and hq % 2 == 0), stop=(kt == nkt - 1), skip_group_check=True)
                return ins
            s.op("pe", fO, r=[("pT", pq), ("vaug", kt)], w=[("ps", ob), ("ps", ob + 1)])
            if kt == nkt - 1:
                Ov = O.rearrange("p (a b) -> p a b", a=4)
                s.op("dve", lambda e: e.reciprocal(out=rden[:, 0:4], in_=Ov[:, :, 128]), r=[("ps", ob), ("ps", ob + 1)], w=["rden"])
                for hq in range(4):
                    s.op("act", lambda e, hq=hq: e.activation(out=ybt[:, (4 * g + hq) * 128:(4 * g + hq + 1) * 128], in_=O[:, hq * 256:hq * 256 + 128],
                                                             func=AF.Identity, scale=rden[:, hq:hq + 1]),
                         r=[("ps", ob), ("ps", ob + 1), "rden"], w=[("ybt", g, hq)])

        LA = 2
        U = []

        def u_pre():
            for i in range(min(LA, len(steps_))):
                S1(i)
        U.append(u_pre)
        for i in range(len(steps_)):
            def u_step(i=i):
                if i + LA < len(steps_):
                    S1(i + LA)
                S2(i)
            U.append(u_step)

        def u_out():
            s.dma("sp", yb_s[rows(j), :], ybt, r=[("ybt", g, hq) for g in range(4) for hq in range(4)])
        U.append(u_out)
        return U

    def interleave(A, B):
        outl = []
        ia = ib_ = 0
        na, nb_ = len(A), len(B)
        while ia < na or ib_ < nb_:
            if ib_ >= nb_ or (ia < na and ia * nb_ <= ib_ * na):
                outl.append(A[ia]); ia += 1
            else:
                outl.append(B[ib_]); ib_ += 1
        return outl

    for u in X_units(0):
        u()
    for j in range(8):
        for u in interleave(Y_units(j), X_units(j + 1) if j < 7 else []):
            u()
    s.barrier()
    if stop_after == "D":
        s.emit(st)
        st.close()
        return nc

    prologue(lambda tt: [(slice(0, 2048), ya_s[rows(tt), :]), (slice(2048, 4096), yb_s[rows(tt), :])], False, nrm=False)

    wtE = {}

    def loadE(nb):
        wtE[nb] = wload(wb[nb % 2][:, 0:16, :], p_a[:, nb * 512:(nb + 1) * 512], 16, 512, lambda ci, sl=nb % 2: ("w", sl, ci))
        wtE[nb] += wload(wb[nb % 2][:, 16:32, :], p_b[:, nb * 512:(nb + 1) * 512], 16, 512, lambda ci, sl=nb % 2: ("w", sl, 2 + ci))
    loadE(0)
    loadE(1)
    for nb in range(8):
        slot = nb % 2
        for pr in range(4):
            tts = [2 * pr, 2 * pr + 1]
            b0 = 4 * (pr % 2)
            lk = []
            for i, tt in enumerate(tts):
                ka, kb_ = rot("ldb", 4), rot("ldb", 4)
                s.dma("sp", ldb[ka], gates_s[rows(tt), nb * 512:(nb + 1) * 512], w=[("ldb", ka)])
                s.dma("sp", ldb[kb_], gates_s[rows(tt), 4096 + nb * 512:4096 + (nb + 1) * 512], w=[("ldb", kb_)])
                lk.append((ka, kb_))

            for ci in range(4):
                def fE(e, tts=tts, b0=b0, slot=slot, ci=ci):
                    ins = None
                    hb = ci // 2
                    for kc in range(8 * ci, 8 * ci + 8):
                        for i, tt in enumerate(tts):
                            ins = e.matmul(bank(b0 + 2 * i + hb), lhsT=AT[:, kc, rows(tt)], rhs=wb[slot][:, kc, :],
                                           start=(kc == 16 * hb), stop=(kc == 16 * hb + 15))
                    return ins
                s.op("pe", fE, r=[wtE[nb][ci]], w=[("ps", b0 + 2 * i + ci // 2) for i in range(2)])
            for i, tt in enumerate(tts):
                ka, kb_ = lk[i]
                fa, fb = rot("stgf", 3), rot("stgf", 3)
                s.op("dve", lambda e, fa=fa, ka=ka, b=b0 + 2 * i: e.tensor_tensor(out=stgf[fa], in0=bank(b), in1=ldb[ka], op=ALU.mult),
                     r=[("ps", b0 + 2 * i), ("ldb", ka)], w=[("stgf", fa)])
                s.op("dve", lambda e, fb=fb, kb_=kb_, b=b0 + 2 * i + 1: e.tensor_tensor(out=stgf[fb], in0=bank(b), in1=ldb[kb_], op=ALU.mult),
                     r=[("ps", b0 + 2 * i + 1), ("ldb", kb_)], w=[("stgf", fb)])
                k = rot("stg", 4)
                s.op("dve", lambda e, fa=fa, fb=fb, k=k: e.tensor_tensor(out=stg[k], in0=stgf[fa], in1=stgf[fb], op=ALU.add),
                     r=[("stgf", fa), ("stgf", fb)], w=[("stg", k)])
                s.dma("sp", merged_s[rows(tt), nb * 512:(nb + 1) * 512], stg[k], r=[("stg", k)])
        if nb + 2 < 8:
            loadE(nb + 2)
    s.barrier()
    if stop_after == "E":
        s.emit(st)
        st.close()
        return nc

    prologue(lambda tt: [(slice(0, 4096), merged_s[rows(tt), :])], False, nrm=False)
    resid_slots = {}

    def make_resid(src, dst, sumsq=False):
        def pre(bi, tts):
            for tt in tts:
                k = rot("ldf", 4)
                s.dma("sp", ldf[k], src[rows(tt), bi * 512:(bi + 1) * 512], w=[("ldf", k)])
                resid_slots[(bi, tt)] = k

        def epi(bi, tt, b, width):
            k = resid_slots[(bi, tt)]
            f = rot("stgf", 3)
            s.op("dve", lambda e: e.tensor_tensor(out=stgf[f], in0=bank(b), in1=ldf[k], op=ALU.add),
                 r=[("ps", b), ("ldf", k)], w=[("stgf", f)])
            if sumsq:
                k2 = rot("stg", 4)
                s.op("act", lambda e: e.activation(out=stg[k2], in_=stgf[f], func=AF.Square, accum_out=ssf[:, tt * 8 + bi:tt * 8 + bi + 1]),
                     r=[("stgf", f)], w=[("stg", k2), "ssf"])
            s.dma("sp", dst[rows(tt), bi * 512:(bi + 1) * 512], stgf[f], r=[("stgf", f)])
        return pre, epi

    pre, epi = make_resid(x_own, x1_s)
    gemm_T(AT, 32, [(w_out[:, i * 512:(i + 1) * 512], 512) for i in range(8)], epi, pre=pre)
    s.barrier()
    if stop_after == "F":
        s.emit(st)
        st.close()
        return nc

    load_gcol(g_x)
    prologue(lambda tt: [x1_s[rows(tt), :]], True)

    def epi_xq(bi, tt, b, width):
        k = rot("stg", 4)
        s.op("act", lambda e: e.copy(out=stg[k], in_=bank(b)), r=[("ps", b)], w=[("stg", k)])
        s.dma("sp", qx_s[rows(tt), bi * 512:(bi + 1) * 512], stg[k], r=[("stg", k)])
    gemm_T(AT, 32, [(xq_w[:, i * 512:(i + 1) * 512], 512) for i in range(2)], epi_xq)
    s.barrier()
    load_gcol(g_mem)
    memT = carve(AT8, 0, [128, 32, 256], BF16)
    prologue(lambda tt: [mem[rows(tt), :]], True, ntt=2, dstT=memT)
    wst_cfg["region"], wst_cfg["off"] = AT8, 32768
    kx_sb = carve(AUX, 0, [128, 2, 1024], BF16)
    vx_aug = carve(AUX, 4096, [128, 2, 4, 258], BF16)
    kxT = carve(AUX, 8448, [128, 8, 256], BF16)
    oxT = carve(AUX, 12544, [128, 8, 1024], BF16)
    s.op("dve", lambda e: e.memset(vx_aug.rearrange("p a b c -> p (a b c)"), 1.0), w=["vx"])

    def epi_mem(bi, mt, b, width):
        if bi < 2:
            s.op("act", lambda e: e.copy(out=kx_sb[:, mt, bi * 512:(bi + 1) * 512], in_=bank(b)), r=[("ps", b)], w=[("kx", mt, bi)])
        else:
            h0 = 2 * (bi - 2)
            s.op("act", lambda e: e.copy(out=vx_aug[:, mt, h0:h0 + 2, 0:256], in_=bank(b).rearrange("p (a b) -> p a b", a=2)),
                 r=[("ps", b)], w=["vx"])
    gemm_T(memT, 32, [(xk_w[:, 0:512], 512), (xk_w[:, 512:1024], 512), (xv_w[:, 0:512], 512), (xv_w[:, 512:1024], 512)],
           epi_mem, ntt=2)
    s.barrier()
    for mt in range(2):
        tp_group([kx_sb[:, mt, c * 128:(c + 1) * 128] for c in range(8)], kxT[:, :, mt * 128:(mt + 1) * 128], w=["kxT"])
    qxt = [carve(AT8, 2048 * i, [128, 1024], BF16) for i in range(2)]
    qxT = carve(AT8, 4096, [128, 8, 128], BF16)
    pxT = carve(AT8, 6144, [128, 8, 128], BF16)
    oxt = carve(AT8, 8192, [128, 1024], BF16)
    for tt in range(8):
        p = tt % 2
        s.dma("sp", qxt[p], qx_s[rows(tt), :], w=[("qxt", p)])
        tp_group([qxt[p][:, c * 128:(c + 1) * 128] for c in range(8)], qxT[:, :, :], r=[("qxt", p)], w=["qxT"])

        def fS(e):
            ins = None
            for hx in range(4):
                for mt in range(2):
                    idx = hx * 2 + mt
                    for dc in range(2):
                        ins = e.matmul(PS[:, idx * 128:(idx + 1) * 128], lhsT=kxT[:, 2 * hx + dc, mt * 128:(mt + 1) * 128],
                                       rhs=qxT[:, 2 * hx + dc, :], start=(dc == 0), stop=(dc == 1))
            return ins
        s.op("pe", fS, r=["kxT", "qxT"], w=[("ps", 0), ("ps", 1)])
        for hb in range(2):
            s.op("act", lambda e, hb=hb: e.activation(out=pxT[:, 4 * hb:4 * hb + 4, :].rearrange("p a b -> p (a b)"), in_=bank(hb),
                                                     func=AF.Exp, scale=0.0625), r=[("ps", hb)], w=[("pxT", hb)])

        def fX(e):
            ins = None
            for hx in range(4):
                for mt in range(2):
                    ins = e.matmul(bank(2 + hx, 257), lhsT=pxT[:, hx * 2 + mt, :], rhs=vx_aug[:, mt, hx, 0:257], start=(mt == 0), stop=(mt == 1))
            return ins
        s.op("pe", fX, r=[("pxT", 0), ("pxT", 1), "vx"], w=[("ps", 2 + hx) for hx in range(4)])
        Ovx = PS[:, 1024:3072].rearrange("p (a b) -> p a b", a=4)
        s.op("dve", lambda e, Ovx=Ovx: e.reciprocal(out=rden[:, 4:8], in_=Ovx[:, :, 256]), r=[("ps", 2 + hx) for hx in range(4)], w=["rdx"])
        for hx in range(4):
            s.op("act", lambda e, hx=hx: e.activation(out=oxt[:, hx * 256:(hx + 1) * 256], in_=bank(2 + hx, 256), func=AF.Identity, scale=rden[:, 4 + hx:5 + hx]),
                 r=[("ps", 2 + hx), "rdx"], w=[("oxt", hx)])
        tp_group([oxt[:, c * 128:(c + 1) * 128] for c in range(8)], oxT[:, :, rows(tt)], r=[("oxt", hx) for hx in range(4)], w=[])
    s.barrier()
    pre, epi = make_resid(x1_s, x2_s)
    gemm_T(oxT, 8, [(xo_w[:, i * 512:(i + 1) * 512], 512) for i in range(8)], epi, pre=pre)
    s.barrier()
    wst_cfg["region"], wst_cfg["off"] = AUX, 0
    if stop_after == "G":
        s.emit(st)
        st.close()
        return nc

    load_gcol(g_ffn)
    prologue(lambda tt: [x2_s[rows(tt), :]], True)
    NBLK = FFN // 256

    wtI = {}

    def loadI(blk):
        sl = blk % 2
        wtI[blk] = wload(wb[sl][:, :, 0:256], w1[:, blk * 256:(blk + 1) * 256], 32, 256, lambda ci, sl=sl: ("w", sl, ci))
        wtI[blk] += wload(wb[sl][:, :, 256:512], w3[:, blk * 256:(blk + 1) * 256], 32, 256, lambda ci, sl=sl: ("w", sl, 2 + ci))
    loadI(0)
    loadI(1)
    for blk in range(NBLK):
        sl = blk % 2
        for hc in range(2):
            b0 = 4 * ((2 * blk + hc) % 2)

            def fI(e, sl=sl, hc=hc, b0=b0):
                ins = None
                for kc in range(32):
                    for a in range(2):
                        lw = wb[sl][:, kc, a * 256 + hc * 128:a * 256 + (hc + 1) * 128]
                        for th in range(2):
                            ins = e.matmul(bank(b0 + 2 * a + th), lhsT=lw, rhs=AT[:, kc, th * 512:(th + 1) * 512], start=(kc == 0), stop=(kc == 31))
                return ins
            s.op("pe", fI, r=wtI[blk], w=[("ps", b0 + i) for i in range(4)])
            for th in range(2):
                f = rot("stgf", 3)
                k = rot("stg", 4)
                s.op("act", lambda e, f=f, b=b0 + th: e.activation(out=stgf[f], in_=bank(b), func=AF.Silu), r=[("ps", b0 + th)], w=[("stgf", f)])
                s.op("dve", lambda e, f=f, k=k, b=b0 + 2 + th: e.tensor_tensor(out=stg[k], in0=bank(b), in1=stgf[f], op=ALU.mult),
                     r=[("ps", b0 + 2 + th), ("stgf", f)], w=[("stg", k)])
                r0 = blk * 256 + hc * 128
                s.dma("sp", fT_s[r0:r0 + 128, th * 512:(th + 1) * 512], stg[k], r=[("stg", k)])
        if blk + 2 < NBLK:
            loadI(blk + 2)
    s.barrier()

    NSJ = 4
    ft = [carve(AT8, 16384 * i, [128, 8, 1024], BF16) for i in range(NSJ)]
    wbJ = [carve(WB, 8192 * i, [128, 8, 512], BF16) for i in range(NSJ)]
    pieces = [(k0, min(k0 + 8, 86)) for k0 in range(0, 86, 8)]
    steps = [(nb, pi) for nb in range(8) for pi in range(len(pieces))]

    wtJ = {}

    def loadJ(si):
        nb, pi = steps[si]
        k0, k1 = pieces[pi]
        sl = si % NSJ
        s.dma("sp", ft[sl][:, 0:k1 - k0, :], fT_s[k0 * 128:k1 * 128, :].rearrange("(kc p) t -> p kc t", p=128), w=[("ft", sl)])
        wtJ[si] = wload(wbJ[sl][:, 0:k1 - k0, :], w2[k0 * 128:k1 * 128, nb * 512:(nb + 1) * 512], k1 - k0, 512, lambda ci, sl=sl: ("wJ", sl, ci))
    wst_cfg["extra"] = [(WB, 32768), (WB, 49152)]
    preJ, epiJ = make_resid(x2_s, x3_s, sumsq=True)
    for si_ in range(NSJ):
        loadJ(si_)
    for si, (nb, pi) in enumerate(steps):
        k0, k1 = pieces[pi]
        sl = si % NSJ
        if pi == len(pieces) - 1:
            preJ(nb, list(range(4)))

        def fJ(e, sl=sl, k0=k0, k1=k1):
            ins = None
            for kc in range(k0, k1):
                for tt in range(8):
                    ins = e.matmul(bank(tt), lhsT=ft[sl][:, kc - k0, rows(tt)], rhs=wbJ[sl][:, kc - k0, :], start=(kc == 0), stop=(kc == 85))
            return ins
        s.op("pe", fJ, r=wtJ[si] + [("ft", sl)], w=[("ps", b) for b in range(8)])
        if si + NSJ < len(steps):
            loadJ(si + NSJ)
        if pi == len(pieces) - 1:
            for tt in range(8):
                if tt == 4:
                    preJ(nb, list(range(4, 8)))
                epiJ(nb, tt, tt, 512)
    s.barrier()

    gf_bc = carve(AT8, 0, [128, 4096], F32)
    oT = [carve(AT8, 16384 + 16384 * i, [128, 4096], F32) for i in range(2)]
    s.dma("sp", gf_bc, g_fin.broadcast_to([128, 4096]), w=["gf"])
    s.op("dve", lambda e: e.tensor_reduce(out=ss[:, 8:16], in_=ssf.rearrange("p (a b) -> p a b", a=8), axis=AX.X, op=ALU.add), r=["ssf"], w=["ss8"])
    s.op("dve", lambda e: e.tensor_scalar(out=rstd8[:, :], in0=ss[:, 8:16], scalar1=1.0 / 4096, scalar2=1e-6, op0=ALU.mult, op1=ALU.add), r=["ss8"], w=["r8"])
    s.op("act", lambda e: e.activation(out=rstd8[:, :], in_=rstd8[:, :], func=AF.Sqrt), r=["r8"], w=["r8"])
    s.op("dve", lambda e: e.reciprocal(out=rstd8[:, :], in_=rstd8[:, :]), r=["r8"], w=["r8"])
    for tt in range(8):
        p = tt % 2
        s.dma("sp", xtW[p], x3_s[rows(tt), :], w=[("xt", p)])
        s.op("dve", lambda e, p=p, tt=tt: e.scalar_tensor_tensor(out=oT[p], in0=xtW[p], scalar=rstd8[:, tt:tt + 1], in1=gf_bc, op0=ALU.mult, op1=ALU.mult),
             r=[("xt", p), "r8", "gf"], w=[("oT", p)])
        s.dma("sp", out[rows(tt), :], oT[p], r=[("oT", p)], is_out=True)
    s.emit(st)
    st.close()
    return nc


def _tile_perm(h):
    own = [2 * j + h for j in range(8)]
    return own


def prep(inputs):
    f32 = lambda a: np.ascontiguousarray(np.asarray(a, dtype=np.float32))
    x = f32(inputs["x"]); mem = f32(inputs["mem"])
    positions = np.asarray(inputs["positions"]).astype(np.int32)
    shared = {
        "w_in": f32(inputs["w_in"][0]), "a_norm_g": f32(inputs["a_norm_g"][0]).reshape(1, 2048),
        "a_spatial_w": f32(inputs["a_spatial_w"][0]), "a_spatial_b": f32(inputs["a_spatial_b"][0]),
        "p_a": f32(inputs["p_a"][0]), "p_b": f32(inputs["p_b"][0]), "w_out": f32(inputs["w_out"][0]),
        "xq_w": f32(inputs["xq_w"][0]), "xk_w": f32(inputs["xk_w"][0]), "xv_w": f32(inputs["xv_w"][0]), "xo_w": f32(inputs["xo_w"][0]),
        "ffn_w1": f32(inputs["ffn_w1"][0]), "ffn_w3": f32(inputs["ffn_w3"][0]), "ffn_w2": f32(inputs["ffn_w2"][0]),
        "norm_mix_g": f32(inputs["norm_mix_g"][0]).reshape(32, 128), "norm_x_g": f32(inputs["norm_x_g"][0]).reshape(32, 128),
        "norm_mem_g": f32(inputs["norm_mem_g"][0]).reshape(32, 128), "norm_ffn_g": f32(inputs["norm_ffn_g"][0]).reshape(32, 128),
        "final_norm_g": f32(inputs["final_norm_g"]).reshape(1, D),
    }
    fb = np.float32(500000.0) ** (-np.arange(0, 32, 2, dtype=np.float32) / np.float32(32))
    fi = np.float32(500000.0) ** (-np.arange(0, 16, 2, dtype=np.float32) / np.float32(16))
    invf = np.ascontiguousarray(np.broadcast_to(np.concatenate([fb, fi]).astype(np.float32)[None, :], (128, 24)))
    tril = np.tril(np.ones((128, 128), dtype=np.float32))
    tri_bias = np.where(tril > 0, 0.0, NEG).astype(np.float32)
    in_maps = []
    for c in range(8):
        b, h = c // 2, c % 2
        own = _tile_perm(h)
        xo = np.ascontiguousarray(x[b].reshape(16, 128, D)[own].reshape(NTOK, D))
        pp = positions[b].reshape(16, 128)
        pos_c = np.ascontiguousarray(np.concatenate([pp, pp[own]], axis=0).T.astype(np.int32))
        mbias = np.empty((128, 256), dtype=np.float32)
        if h == 0:
            mbias[:, 0:128] = tri_bias
            mbias[:, 128:256] = NEG
        else:
            mbias[:, 0:128] = 0.0
            mbias[:, 128:256] = tri_bias
        m = dict(shared)
        m.update({"x_full": x[b], "x_own": xo, "mem": mem[b], "pos": pos_c, "maskb": mbias, "tril": tril, "invf": invf})
        in_maps.append(m)
    return in_maps


def kernel(**inputs):
    nc = build_program()
    in_maps = prep(inputs)
    res = run_bass_kernel_spmd(nc, in_maps, core_ids=list(range(8)))
    outp = np.empty((4, 2048, D), dtype=np.float32)
    for c in range(8):
        b, h = c // 2, c % 2
        o = np.asarray(res.results[c]["out"]).reshape(8, 128, D)
        for j in range(8):
            outp[b, (2 * j + h) * 128:(2 * j + h + 1) * 128, :] = o[j]
    return outp
```
